# Optimizing a Trainium2 kernel written in Bass

```python
import jax, jax.numpy as jnp
from jax import lax
import numpy as np

D_MODEL = 2048
BATCH = 4
SEQ = 8192
DEPTH = 1

D_FF = 5632
CONV_WIDTH = D_MODEL
CONV_K = 3
N_Q_HEADS = 32
N_KV_HEADS = 4
HEAD_DIM = 64
Q_GROUP = N_Q_HEADS // N_KV_HEADS
ATTN_WIDTH = N_Q_HEADS * HEAD_DIM
KV_WIDTH = N_KV_HEADS * HEAD_DIM
WINDOW = 128
BLOCK = 128
RMS_EPS = 1e-5
ATTN_SCALE = HEAD_DIM ** -0.5
IN_COLS = (CONV_WIDTH, CONV_WIDTH, CONV_WIDTH, ATTN_WIDTH, KV_WIDTH, KV_WIDTH, D_MODEL, D_MODEL)
D_IN_PROJ = sum(IN_COLS)

kernel_name = "hybrid_gated_shortconv_swa_macaron"


def rms_norm(x, g):
    x32 = x.astype(jnp.float32)
    y = x32 * lax.rsqrt(jnp.mean(x32 * x32, axis=-1, keepdims=True) + RMS_EPS)
    return (y * g.astype(jnp.float32)).astype(x.dtype)


def swiglu(x, w_gate, w_up, w_down):
    return (jax.nn.silu(x @ w_gate) * (x @ w_up)) @ w_down


def short_gated_conv(b, c, xc, w_conv):
    u = c * xc
    seq = u.shape[1]
    u_pad = jnp.pad(u, ((0, 0), (CONV_K - 1, 0), (0, 0)))
    conv = w_conv[CONV_K - 1] * u
    for tap in range(CONV_K - 1):
        conv = conv + w_conv[tap] * u_pad[:, tap:tap + seq]
    return b * conv


def alibi_slopes(n_heads):
    return 2.0 ** (-8.0 * jnp.arange(1, n_heads + 1, dtype=jnp.float32) / n_heads)


def _swa_single(q, k, v, sinks):
    seq = q.shape[0]
    n_blk = seq // BLOCK
    q = q.reshape(n_blk, BLOCK, N_KV_HEADS, Q_GROUP, HEAD_DIM)
    k = k.reshape(n_blk, BLOCK, N_KV_HEADS, HEAD_DIM)
    v = v.reshape(n_blk, BLOCK, N_KV_HEADS, HEAD_DIM)
    prev = lambda t: jnp.concatenate([jnp.zeros_like(t[:1]), t[:-1]], axis=0)
    k_win = jnp.concatenate([prev(k), k], axis=1)
    v_win = jnp.concatenate([prev(v), v], axis=1)
    scores = jnp.einsum('nqhgd,nshd->nhgqs', q, k_win).astype(jnp.float32) * ATTN_SCALE
    qi = jnp.arange(BLOCK)[:, None]
    kj = jnp.arange(2 * BLOCK)[None, :]
    dist = qi - kj + BLOCK
    band = (dist >= 0) & (dist < WINDOW)
    key_pos = jnp.arange(n_blk)[:, None] * BLOCK - BLOCK + kj
    valid = band[None] & (key_pos >= 0)[:, None, :]
    slopes = alibi_slopes(N_Q_HEADS).reshape(N_KV_HEADS, Q_GROUP)
    bias = -slopes[:, :, None, None] * dist.astype(jnp.float32)
    scores = jnp.where(valid[:, None, None], scores + bias[None], -jnp.inf)
    sink = sinks.astype(jnp.float32).reshape(N_KV_HEADS, Q_GROUP)[None, :, :, None, None]
    m = jnp.maximum(jnp.max(scores, axis=-1, keepdims=True), sink)
    p = jnp.exp(scores - m)
    denom = jnp.sum(p, axis=-1, keepdims=True) + jnp.exp(sink - m)
    p = (p / denom).astype(v.dtype)
    out = jnp.einsum('nhgqs,nshd->nqhgd', p, v_win)
    return out.reshape(seq, ATTN_WIDTH)


def sliding_window_attention(q, k, v, sinks):
    return lax.map(lambda qkv: _swa_single(qkv[0], qkv[1], qkv[2], sinks), (q, k, v))


def gated_parallel_mixer(u, w_in, w_conv, w_conv_out, attn_sinks, w_attn_out, w_out):
    proj = u @ w_in
    split_idx = [int(i) for i in np.cumsum(IN_COLS)[:-1]]
    b, c, xc, q, k, v, g_conv, g_attn = jnp.split(proj, split_idx, axis=-1)
    y_conv = short_gated_conv(b, c, xc, w_conv) @ w_conv_out
    y_attn = sliding_window_attention(q, k, v, attn_sinks) @ w_attn_out
    merged = jax.nn.sigmoid(g_conv) * y_conv + jax.nn.sigmoid(g_attn) * y_attn
    return merged @ w_out


def setup_inputs(seed: int = 0) -> dict:
    key = jax.random.key(seed)
    ks = jax.random.split(key, 20)
    f32 = jnp.float32
    nrm = lambda k, shape, fan_in: jax.random.normal(k, shape, f32) * (fan_in ** -0.5)
    gain = lambda k, shape: 1.0 + 0.02 * jax.random.normal(k, shape, f32)
    L = DEPTH
    return {
        "x": jax.random.normal(ks[0], (BATCH, SEQ, D_MODEL), f32),
        "norm_ffn1": gain(ks[1], (L, D_MODEL)),
        "w_gate1": nrm(ks[2], (L, D_MODEL, D_FF), D_MODEL),
        "w_up1": nrm(ks[3], (L, D_MODEL, D_FF), D_MODEL),
        "w_down1": nrm(ks[4], (L, D_FF, D_MODEL), D_FF),
        "norm_mix": gain(ks[5], (L, D_MODEL)),
        "w_in": nrm(ks[6], (L, D_MODEL, D_IN_PROJ), D_MODEL),
        "w_conv": nrm(ks[7], (L, CONV_K, CONV_WIDTH), CONV_K),
        "w_conv_out": nrm(ks[8], (L, CONV_WIDTH, D_MODEL), CONV_WIDTH),
        "attn_sinks": jax.random.normal(ks[9], (L, N_Q_HEADS), f32),
        "w_attn_out": nrm(ks[10], (L, ATTN_WIDTH, D_MODEL), ATTN_WIDTH),
        "w_out": nrm(ks[11], (L, D_MODEL, D_MODEL), D_MODEL),
        "norm_ffn2": gain(ks[12], (L, D_MODEL)),
        "w_gate2": nrm(ks[13], (L, D_MODEL, D_FF), D_MODEL),
        "w_up2": nrm(ks[14], (L, D_MODEL, D_FF), D_MODEL),
        "w_down2": nrm(ks[15], (L, D_FF, D_MODEL), D_FF),
        "norm_final": gain(ks[16], (D_MODEL,)),
    }


def reference(x, norm_ffn1, w_gate1, w_up1, w_down1, norm_mix, w_in, w_conv, w_conv_out,
              attn_sinks, w_attn_out, w_out, norm_ffn2, w_gate2, w_up2, w_down2, norm_final):
    h = x
    for l in range(DEPTH):
        h = h + 0.5 * swiglu(rms_norm(h, norm_ffn1[l]), w_gate1[l], w_up1[l], w_down1[l])
        h = h + gated_parallel_mixer(rms_norm(h, norm_mix[l]), w_in[l], w_conv[l], w_conv_out[l],
                                     attn_sinks[l], w_attn_out[l], w_out[l])
        h = h + 0.5 * swiglu(rms_norm(h, norm_ffn2[l]), w_gate2[l], w_up2[l], w_down2[l])
    return rms_norm(h, norm_final)
```

```python
import contextlib
import numpy as np
import concourse.bass as bass
import concourse.mybir as mybir
from concourse.bass_utils import run_bass_kernel_spmd

F32 = mybir.dt.float32
BF16 = mybir.dt.bfloat16
AF = mybir.ActivationFunctionType
ALU = mybir.AluOpType

D = 2048
DFF = 5632
NCH = 16
NFF = 44
TT = 512
SEQ = 8192
BATCH = 4
NCORES = 8
TOK_CORE = SEQ * BATCH // NCORES
NSLOT = 3
SLAB_E = 8192
EPS = 1e-5
NEG = -1.0e7
B0, C0, X0, Q0, K0, V0, GC0, GA0 = 0, 2048, 4096, 6144, 8192, 8448, 8704, 10752
NVEC = 129
NCONST = NVEC + 128 + 256

ENGS = ('pe', 'act', 'dve', 'pool', 'sp')


class Op:
    __slots__ = ('eng', 'fn', 'deps', 'signal', 'val', 'sem', 'dma_key', 'epoch')

    def __init__(self, eng, fn, dma_key, epoch):
        self.eng = eng
        self.fn = fn
        self.deps = []
        self.signal = False
        self.val = None
        self.sem = None
        self.dma_key = dma_key
        self.epoch = epoch


class Prog:
    def __init__(self):
        self.ops = {e: [] for e in ENGS}
        self.last_w = {}
        self.readers = {}
        self.epoch = 0
        self.dma_count = {}
        self.nops = 0

    def add(self, eng, fn, reads=(), writes=(), dma_key=None):
        op = Op(eng, fn, dma_key, self.epoch)
        deps = {}
        for r in reads:
            w = self.last_w.get(r)
            if w is not None:
                deps[id(w)] = w
        for r in writes:
            w = self.last_w.get(r)
            if w is not None:
                deps[id(w)] = w
            rd = self.readers.get(r)
            if rd:
                for k, v in rd.items():
                    if k == '_dma':
                        for x in v:
                            deps[id(x)] = x
                    else:
                        deps[id(v)] = v
        for r in reads:
            rd = self.readers.setdefault(r, {})
            if dma_key is not None:
                rd.setdefault('_dma', []).append(op)
            else:
                rd[eng] = op
        for r in writes:
            self.last_w[r] = op
            self.readers[r] = {}
        for d in deps.values():
            if d is op:
                continue
            if d.dma_key is None and dma_key is None and d.eng == 'pe' and eng == 'pe':
                continue
            op.deps.append(d)
            d.signal = True
        if dma_key is not None:
            self.dma_count[dma_key] = self.dma_count.get(dma_key, 0) + 1
            op.val = 16 * self.dma_count[dma_key]
        self.ops[eng].append(op)
        self.nops += 1
        return op

    def emit(self, nc):
        sem_objs = {}
        with contextlib.ExitStack() as st:
            def getsem(key):
                if key not in sem_objs:
                    nm = "s_" + "_".join(str(k) for k in key)
                    sem_objs[key] = st.enter_context(nc.semaphore(nm))
                return sem_objs[key]
            for e in ENGS:
                cnt = {}
                for op in self.ops[e]:
                    if op.dma_key is not None:
                        op.sem = getsem(('d',) + tuple(op.dma_key))
                    elif op.signal:
                        cnt[op.epoch] = cnt.get(op.epoch, 0) + 1
                        op.val = cnt[op.epoch]
                        op.sem = getsem((e, op.epoch))
            self.nsems = len(sem_objs)
            block = st.enter_context(nc.Block())

            def run(engine, e):
                waited = {}
                for op in self.ops[e]:
                    need = {}
                    for d in op.deps:
                        k = id(d.sem)
                        if k not in need or need[k][1] < d.val:
                            need[k] = (d.sem, d.val)
                    for k, (sem, val) in need.items():
                        if waited.get(k, 0) < val:
                            engine.wait_ge(sem, val)
                            waited[k] = val
                    if op.fn is None:
                        continue
                    inst = op.fn(engine)
                    if op.dma_key is not None:
                        inst.then_inc(op.sem, 16)
                    elif op.signal:
                        inst.then_inc(op.sem, 1)

            @block.tensor
            def _(eng):
                run(eng, 'pe')

            @block.scalar
            def _(eng):
                run(eng, 'act')

            @block.vector
            def _(eng):
                run(eng, 'dve')

            @block.gpsimd
            def _(eng):
                run(eng, 'pool')

            @block.sync
            def _(eng):
                run(eng, 'sp')


def slab_table():
    tab = []
    for f in (1, 2):
        for i in range(22):
            tab.append((f"f{f}gu{i}", 2 * 16 * 256))
        for mg in range(8):
            for q in range(2):
                tab.append((f"f{f}d{mg}_{q}", 22 * 256))
        if f == 1:
            tab.append(("ka", 16 * 512))
            tab.append(("kb", 16 * 512))
            tab.append(("v", 16 * 256))
            for g in range(4):
                tab.append((f"q{g}", 16 * 512))
            for i in range(8):
                tab.append((f"gaao{i}", 2 * 16 * 256))
            for j in range(16):
                tab.append((f"conv{j}", 16 * 384))
            for i in range(8):
                tab.append((f"gcco{i}", 2 * 16 * 256))
            for i in range(4):
                tab.append((f"wo{i}", 16 * 512))
    offs = {}
    off = 0
    order = []
    for nm, e in tab:
        offs[nm] = (off, e, len(order))
        order.append(nm)
        off += e
    return offs, order, off


SLAB_OFFS, SLAB_ORDER, SLAB_TOT = slab_table()


def build_wslab(inp):
    out = np.zeros((128, SLAB_TOT), dtype=np.float32)

    def kview(W):
        K = W.shape[0] // 128
        return W.reshape(K, 128, -1).transpose(1, 0, 2)

    def put(nm, arr):
        off, e, _ = SLAB_OFFS[nm]
        a = np.ascontiguousarray(arr).reshape(128, -1)
        assert a.shape[1] == e, (nm, a.shape, e)
        out[:, off:off + e] = a

    for f, (wg, wu, wd) in ((1, ("w_gate1", "w_up1", "w_down1")), (2, ("w_gate2", "w_up2", "w_down2"))):
        g3, u3, d3 = kview(inp[wg][0]), kview(inp[wu][0]), kview(inp[wd][0])
        for i in range(22):
            put(f"f{f}gu{i}", np.stack([g3[:, :, i * 256:(i + 1) * 256], u3[:, :, i * 256:(i + 1) * 256]], axis=1))
        for mg in range(8):
            for q in range(2):
                put(f"f{f}d{mg}_{q}", d3[:, q * 22:(q + 1) * 22, mg * 256:(mg + 1) * 256])
    w3 = kview(inp["w_in"][0])
    ka = np.zeros((128, 16, 4, 128), np.float32)
    kb = np.zeros((128, 16, 4, 128), np.float32)
    for g in range(4):
        ka[:, :, g, 0:64] = w3[:, :, K0 + g * 64:K0 + (g + 1) * 64]
        kb[:, :, g, 64:128] = w3[:, :, K0 + g * 64:K0 + (g + 1) * 64]
    put("ka", ka)
    put("kb", kb)
    put("v", w3[:, :, V0:V0 + 256])
    for g in range(4):
        put(f"q{g}", w3[:, :, Q0 + g * 512:Q0 + (g + 1) * 512])
    ao3 = kview(inp["w_attn_out"][0])
    co3 = kview(inp["w_conv_out"][0])
    wo3 = kview(inp["w_out"][0])
    for i in range(8):
        put(f"gaao{i}", np.stack([w3[:, :, GA0 + i * 256:GA0 + (i + 1) * 256], ao3[:, :, i * 256:(i + 1) * 256]], axis=1))
        put(f"gcco{i}", np.stack([w3[:, :, GC0 + i * 256:GC0 + (i + 1) * 256], co3[:, :, i * 256:(i + 1) * 256]], axis=1))
    for j in range(16):
        put(f"conv{j}", np.concatenate([w3[:, :, C0 + j * 128:C0 + (j + 1) * 128],
                                        w3[:, :, X0 + j * 128:X0 + (j + 1) * 128],
                                        w3[:, :, B0 + j * 128:B0 + (j + 1) * 128]], axis=2))
    for i in range(4):
        put(f"wo{i}", wo3[:, :, i * 512:(i + 1) * 512])
    return out


def build_consts(inp, seq_start):
    c = np.zeros((128, NCONST), np.float32)
    for n, nm in enumerate(("norm_ffn1", "norm_mix", "norm_ffn2", "norm_final")):
        v = inp[nm].reshape(-1)
        c[:, n * 16:(n + 1) * 16] = v.reshape(16, 128).T
    wc = inp["w_conv"][0]
    for k in range(3):
        c[:, 64 + k * 16:64 + (k + 1) * 16] = wc[k].reshape(16, 128).T
    sk = inp["attn_sinks"][0]
    for g in range(4):
        for j in range(4):
            c[0:64, 112 + 4 * g + j] = sk[8 * g + 2 * j]
            c[64:128, 112 + 4 * g + j] = sk[8 * g + 2 * j + 1]
    c[:, 128] = -30000.0 if seq_start else 0.0
    c[:, NVEC:NVEC + 128] = np.eye(128, dtype=np.float32)
    s = np.arange(128)[:, None]
    q = np.arange(128)[None, :]
    dprev = q - s + 128
    dcur = q - s
    c[:, NVEC + 128:NVEC + 256] = np.where(dprev < 128, -dprev, NEG)
    c[:, NVEC + 256:NVEC + 384] = np.where(dcur >= 0, -dcur, NEG)
    return c


def build_program(nt=8):
    nc = bass.Bass("TRN2", target_bir_lowering=False)
    nrows = 128 + nt * TT
    xs = nc.dram_tensor("xs", [nrows, D], F32, kind="ExternalInput").ap()
    wsl = nc.dram_tensor("wslab", [128, SLAB_TOT], F32, kind="ExternalInput").ap()
    cst = nc.dram_tensor("consts", [128, NCONST], F32, kind="ExternalInput").ap()
    out = nc.dram_tensor("out", [nt * TT, D], F32, kind="ExternalOutput").ap()
    scr = nc.dram_tensor("scr", [128, SLAB_TOT], BF16, kind="Internal").ap()

    P = Prog()
    with contextlib.ExitStack() as st:
        def sb(name, shape, dt):
            return st.enter_context(nc.sbuf_tensor(name, shape, dt))
        ps = [st.enter_context(nc.psum_tensor(f"ps{i}", [128, 512], F32)) for i in range(8)]
        h = sb("h", [128, NCH * TT], F32)
        u = sb("u", [128, NCH * TT], BF16)
        rg = sb("rg", [128, 16384], F32)
        tp = sb("tp", [128, 5120], F32)
        slab = sb("slab", [128, NSLOT * SLAB_E], BF16)
        kA = sb("kA", [128, 4 * 640], BF16)
        kB = sb("kB", [128, 4 * 640], BF16)
        Vt = sb("Vt", [128, 5 * 256], BF16)
        uch = sb("uch", [128, 32], F32)
        cs = sb("cs", [128, NCONST], F32)
        ones = sb("ones", [128, 128], F32)
        onesb = sb("onesb", [128, 64], BF16)
        sinkexp = sb("sinkexp", [128, 16], F32)

        rgb = rg[:, :].bitcast(BF16)
        tpb = tp[:, :].bitcast(BF16)
        ident = cs[:, NVEC:NVEC + 128]
        Dm = [cs[:, NVEC + 128:NVEC + 256], cs[:, NVEC + 256:NVEC + 384]]
        hmask = cs[:, 128:129]
        kA3 = kA[:, :].rearrange("p (g t) -> p g t", t=640)
        kB3 = kB[:, :].rearrange("p (g t) -> p g t", t=640)
        V3 = Vt[:, :].rearrange("p (b c) -> p b c", c=256)

        def hc(c, T=TT, lo=0):
            return h[:, c * TT + lo:c * TT + T]

        def uc_(c, T=TT, lo=0):
            return u[:, c * TT + lo:c * TT + T]

        def rg_b(cell, T=TT, lo=0):
            return rgb[:, cell * 512 + lo:cell * 512 + T]

        def rg_f(cell, n=TT, lo=0):
            return rg[:, cell * 256 + lo:cell * 256 + n]

        def tp_f(cell, n=TT, lo=0):
            return tp[:, cell * 256 + lo:cell * 256 + n]

        def tp_b(cell, n=TT, lo=0):
            return tpb[:, cell * 512 + lo:cell * 512 + n]

        RG = lambda a, n=1: [('rg', a + i) for i in range(n)]
        TP = lambda a, n=1: [('tp', a + i) for i in range(n)]
        PS = lambda b: [('ps', b)]

        state = dict(slab=0, xin=0, ost=0, bank4=0, bankt=0, outd=0)

        P.add('sp', lambda e: e.dma_start(out=cs[:, :], in_=cst), writes=['cs'], dma_key=('cs',))
        P.add('pool', lambda e: e.memset(ones[:, :], 1.0), writes=['ones'])
        P.add('pool', lambda e: e.memset(onesb[:, :], 1.0), writes=['onesb'])
        P.add('pool', lambda e: e.memset(uch[:, :], 0.0), writes=[('uch', j) for j in range(16)])
        P.add('pool', lambda e: e.memset(kA[:, :], 0.0), writes=[('kA', b) for b in range(5)])
        P.add('pool', lambda e: e.memset(kB[:, :], 0.0), writes=[('kB', b) for b in range(5)])
        P.add('pool', lambda e: e.memset(Vt[:, :], 0.0), writes=[('V', b) for b in range(5)])
        P.add('act', lambda e: e.activation(out=sinkexp[:, :], in_=cs[:, 112:128], func=AF.Exp),
              reads=['cs'], writes=['sinkexp'])

        for i, nm in enumerate(SLAB_ORDER):
            off, e_, _ = SLAB_OFFS[nm]
            P.add('pool', lambda e, off=off, e_=e_: e.dma_start(out=scr[:, off:off + e_], in_=wsl[:, off:off + e_]),
                  reads=[], writes=[('scr', nm), ('castsem', i % 8)], dma_key=('cast', i % 8))

        def load_slab(nm):
            off, e_, _ = SLAB_OFFS[nm]
            slot = state['slab'] % NSLOT
            state['slab'] += 1
            base = slot * SLAB_E
            P.add('sp', lambda e: e.dma_start(out=slab[:, base:base + e_], in_=scr[:, off:off + e_]),
                  reads=[('scr', nm)], writes=[('slab', slot)], dma_key=('slab', slot))
            return slab[:, base:base + e_], [('slab', slot)]

        def rot4():
            b = state['bank4'] % 4
            state['bank4'] += 1
            return b

        def rott():
            b = 4 + state['bankt'] % 4
            state['bankt'] += 1
            return b

        def load_x(row0, T):
            nb = T // 128
            h3 = h[:, :].rearrange("p (c t) -> p c t", t=TT)
            for b in range(nb):
                slot = state['xin'] % 2
                state['xin'] += 1
                cell = 44 + 8 * slot
                dst = rg_f(cell, 2048)
                r0 = row0 + b * 128
                P.add('sp', lambda e, dst=dst, r0=r0: e.dma_start(out=dst, in_=xs[r0:r0 + 128, :]),
                      writes=RG(cell, 8), dma_key=('xin', slot))
                for cg in range(4):
                    bank = rott()
                    for j in range(4):
                        c = 4 * cg + j
                        P.add('pe', lambda e, bank=bank, j=j, c=c, cell=cell:
                              e.transpose(ps[bank][:, j * 128:(j + 1) * 128], rg_f(cell, (c + 1) * 128, c * 128), ident),
                              reads=RG(cell, 8) + ['cs'], writes=PS(bank))
                    src = ps[bank][:, :].rearrange("p (c t) -> p c t", t=128)
                    dsth = h3[:, 4 * cg:4 * cg + 4, b * 128:(b + 1) * 128]
                    eng = 'act' if cg % 2 == 0 else 'dve'
                    if eng == 'act':
                        fn = lambda e, src=src, dsth=dsth: e.activation(out=dsth, in_=src, func=AF.Copy)
                    else:
                        fn = lambda e, src=src, dsth=dsth: e.tensor_copy(out=dsth, in_=src)
                    P.add(eng, fn, reads=PS(bank), writes=[('h', 4 * cg + j) for j in range(4)])

        def norm(T, nidx, final=False):
            SQ = [6, 8, 10]
            RS = 12
            for c in range(NCH):
                s = SQ[c % 3]
                P.add('act', lambda e, s=s, c=c: e.activation(out=tp_f(s, T), in_=hc(c, T), func=AF.Square),
                      reads=[('h', c)], writes=TP(s, 2))
                P.add('pe', lambda e, s=s, c=c: e.matmul(ps[4][:, :T], ones[:, :], tp_f(s, T), start=(c == 0), stop=(c == NCH - 1)),
                      reads=TP(s, 2) + ['ones'], writes=PS(4))
            P.add('dve', lambda e: e.tensor_scalar(out=tp_f(RS, T), in0=ps[4][:, :T], scalar1=1.0 / D, scalar2=EPS,
                                                    op0=ALU.mult, op1=ALU.add), reads=PS(4), writes=TP(RS, 2))
            P.add('act', lambda e: e.activation(out=tp_f(RS, T), in_=tp_f(RS, T), func=AF.Sqrt), reads=TP(RS, 2), writes=TP(RS, 2))
            P.add('dve', lambda e: e.reciprocal(out=tp_f(RS, T), in_=tp_f(RS, T)), reads=TP(RS, 2), writes=TP(RS, 2))
            for c in range(NCH):
                g_ap = cs[:, nidx * 16 + c:nidx * 16 + c + 1]
                if final:
                    P.add('dve', lambda e, c=c, g_ap=g_ap: e.scalar_tensor_tensor(
                        out=hc(c, T), in0=hc(c, T), scalar=g_ap, in1=tp_f(RS, T), op0=ALU.mult, op1=ALU.mult),
                        reads=[('h', c), 'cs'] + TP(RS, 2), writes=[('h', c)])
                else:
                    P.add('dve', lambda e, c=c, g_ap=g_ap: e.scalar_tensor_tensor(
                        out=uc_(c, T), in0=hc(c, T), scalar=g_ap, in1=tp_f(RS, T), op0=ALU.mult, op1=ALU.mult),
                        reads=[('h', c), 'cs'] + TP(RS, 2), writes=[('u', c)])

        def dual_slab_pass(T, prefix, n, rhs_b, rhs_res_b, evac):
            for i in range(n):
                sl, sres = load_slab(f"{prefix}{i}")
                s4 = sl.rearrange("p (a k n) -> p a k n", a=2, k=16)
                for j in range(2):
                    m = 2 * i + j
                    pb = 2 * (m % 2)
                    for a in range(2):
                        for k in range(NCH):
                            if a == 0:
                                rhs, rres = uc_(k, T), [('u', k)]
                            else:
                                rhs, rres = rhs_b(k), rhs_res_b(k)
                            P.add('pe', lambda e, pb=pb, a=a, k=k, j=j, s4=s4, rhs=rhs: e.matmul(
                                ps[pb + a][:, :T], s4[:, a, k, j * 128:(j + 1) * 128], rhs, start=(k == 0), stop=(k == NCH - 1)),
                                reads=sres + rres, writes=PS(pb + a))
                    evac(m, pb, pb + 1)

        def ffn(T, f, nidx):
            norm(T, nidx)
            SG = [0, 2, 4]

            def evac(m, ba, bb):
                s = SG[m % 3]
                P.add('act', lambda e: e.activation(out=tp_f(s, T), in_=ps[ba][:, :T], func=AF.Silu),
                      reads=PS(ba), writes=TP(s, 2))
                P.add('dve', lambda e: e.tensor_tensor(out=rg_b(m, T), in0=tp_f(s, T), in1=ps[bb][:, :T], op=ALU.mult),
                      reads=TP(s, 2) + PS(bb), writes=RG(m))
            dual_slab_pass(T, f"f{f}gu", 22, lambda k: uc_(k, T), lambda k: [('u', k)], evac)
            for mg in range(8):
                pb = 2 * (mg % 2)
                for q in range(2):
                    sl, sres = load_slab(f"f{f}d{mg}_{q}")
                    s3 = sl.rearrange("p (k n) -> p k n", k=22)
                    for k in range(22):
                        kk = q * 22 + k
                        for j in range(2):
                            P.add('pe', lambda e, pb=pb, j=j, k=k, kk=kk, s3=s3, q=q: e.matmul(
                                ps[pb + j][:, :T], s3[:, k, j * 128:(j + 1) * 128], rg_b(kk, T),
                                start=(kk == 0), stop=(kk == NFF - 1)),
                                reads=sres + RG(kk), writes=PS(pb + j))
                for j in range(2):
                    m = 2 * mg + j
                    P.add('dve', lambda e, pb=pb, j=j, m=m: e.scalar_tensor_tensor(
                        out=hc(m, T), in0=ps[pb + j][:, :T], scalar=0.5, in1=hc(m, T), op0=ALU.mult, op1=ALU.add),
                        reads=PS(pb + j) + [('h', m)], writes=[('h', m)])

        def attention(b, g, first_block):
            qs = g % 2
            q3 = rgb[:, (48 + 4 * qs) * 512:(48 + 4 * qs + 4) * 512].rearrange("p (c t) -> p c t", t=TT)[:, :, b * 128:(b + 1) * 128]
            qres = RG(48 + 4 * qs, 4)
            for par in range(2):
                kb3 = kA3 if par == 0 else kB3
                kname = 'kA' if par == 0 else 'kB'
                for ty in range(2):
                    i = par * 2 + ty
                    bank = 4 + i
                    blk = b + ty
                    P.add('pe', lambda e, bank=bank, kb3=kb3, blk=blk: e.matmul(
                        ps[bank][:, :], kb3[:, g, blk * 128:(blk + 1) * 128], q3, start=True, stop=True),
                        reads=[(kname, blk)] + qres, writes=PS(bank))
                    for j in range(4):
                        hh = 8 * g + 2 * j + par
                        slope = float(2.0 ** (-(hh + 1) / 4.0))
                        P.add('dve', lambda e, i=i, j=j, ty=ty, slope=slope, bank=bank: e.scalar_tensor_tensor(
                            out=tp_f(2 * i, (j + 1) * 128, j * 128), in0=Dm[ty], scalar=slope,
                            in1=ps[bank][:, j * 128:(j + 1) * 128], op0=ALU.mult, op1=ALU.add),
                            reads=PS(bank) + ['cs'], writes=TP(2 * i, 2))
                    if first_block and ty == 0:
                        fn = lambda e, i=i: e.activation(out=tp_b(8 + i, 512), in_=tp_f(2 * i, 512), func=AF.Exp, bias=hmask, scale=1.0)
                    else:
                        fn = lambda e, i=i: e.activation(out=tp_b(8 + i, 512), in_=tp_f(2 * i, 512), func=AF.Exp)
                    P.add('act', fn, reads=TP(2 * i, 2) + ['cs'], writes=TP(8 + i))
            for par in range(2):
                lo, hi = par * 64, (par + 1) * 64
                for ty in range(2):
                    i = par * 2 + ty
                    blk = b + ty
                    P.add('pe', lambda e, lo=lo, hi=hi, i=i, blk=blk, ty=ty: e.matmul(
                        ps[2][lo:hi, :], V3[:, blk, g * 64:(g + 1) * 64], tp_b(8 + i, 512), start=(ty == 0), stop=(ty == 1)),
                        reads=[('V', blk)] + TP(8 + i), writes=PS(2))
                for ty in range(2):
                    i = par * 2 + ty
                    P.add('pe', lambda e, lo=lo, hi=hi, i=i, ty=ty: e.matmul(
                        ps[3][lo:hi, :], onesb[:, 0:64], tp_b(8 + i, 512), start=(ty == 0), stop=(ty == 1)),
                        reads=['onesb'] + TP(8 + i), writes=PS(3))
            for j in range(4):
                P.add('dve', lambda e, j=j: e.tensor_scalar(
                    out=tp_f(12, (j + 1) * 128, j * 128), in0=ps[3][:, j * 128:(j + 1) * 128],
                    scalar1=sinkexp[:, 4 * g + j:4 * g + j + 1], scalar2=None, op0=ALU.add),
                    reads=PS(3) + ['sinkexp'], writes=TP(12, 2))
            P.add('dve', lambda e: e.reciprocal(out=tp_f(14, 512), in_=tp_f(12, 512)), reads=TP(12, 2), writes=TP(14, 2))
            o3 = rgb[:, (32 + 4 * g) * 512:(32 + 4 * g + 4) * 512].rearrange("p (c t) -> p c t", t=TT)[:, :, b * 128:(b + 1) * 128]
            P.add('dve', lambda e: e.tensor_tensor(
                out=o3, in0=ps[2][:, :].rearrange("p (c t) -> p c t", t=128),
                in1=tp_f(14, 512).rearrange("p (c t) -> p c t", t=128), op=ALU.mult),
                reads=PS(2) + TP(14, 2), writes=RG(32 + 4 * g, 4))

        def mixer(T, halo, first):
            nb = T // 128
            norm(T, 1)
            for nm, kbuf3, kname in (("ka", kA3, 'kA'), ("kb", kB3, 'kB')):
                sl, sres = load_slab(nm)
                s3 = sl.rearrange("p (k n) -> p k n", k=16)
                for g in range(4):
                    bank = rot4()
                    for k in range(NCH):
                        P.add('pe', lambda e, bank=bank, k=k, g=g, s3=s3: e.matmul(
                            ps[bank][:, :T], s3[:, k, g * 128:(g + 1) * 128], uc_(k, T), start=(k == 0), stop=(k == NCH - 1)),
                            reads=sres + [('u', k)], writes=PS(bank))
                    P.add('act', lambda e, bank=bank, g=g, kbuf3=kbuf3: e.activation(
                        out=kbuf3[:, g, 128:128 + T], in_=ps[bank][:, :T], func=AF.Copy, scale=0.125),
                        reads=PS(bank), writes=[(kname, 1 + b) for b in range(nb)])
            sl, sres = load_slab("v")
            s3 = sl.rearrange("p (k n) -> p k n", k=16)
            for b in range(nb):
                bank = rot4()
                for k in range(NCH):
                    P.add('pe', lambda e, bank=bank, k=k, b=b, s3=s3: e.matmul(
                        ps[bank][:, 0:256], uc_(k, (b + 1) * 128, b * 128), s3[:, k, :], start=(k == 0), stop=(k == NCH - 1)),
                        reads=sres + [('u', k)], writes=PS(bank))
                P.add('dve', lambda e, bank=bank, b=b: e.tensor_copy(out=V3[:, 1 + b, :], in_=ps[bank][:, 0:256]),
                      reads=PS(bank), writes=[('V', 1 + b)])
            if not halo:
                for g in range(4):
                    sl, sres = load_slab(f"q{g}")
                    s3 = sl.rearrange("p (k n) -> p k n", k=16)
                    qs = g % 2
                    for j in range(4):
                        bank = j % 2
                        for k in range(NCH):
                            P.add('pe', lambda e, bank=bank, k=k, j=j, s3=s3: e.matmul(
                                ps[bank][:, :T], s3[:, k, j * 128:(j + 1) * 128], uc_(k, T), start=(k == 0), stop=(k == NCH - 1)),
                                reads=sres + [('u', k)], writes=PS(bank))
                        cell = 48 + 4 * qs + j
                        P.add('act', lambda e, bank=bank, cell=cell: e.activation(out=rg_b(cell, T), in_=ps[bank][:, :T], func=AF.Copy),
                              reads=PS(bank), writes=RG(cell))
                    for b in range(nb):
                        attention(b, g, first and b == 0)
                SG = [0, 2, 4]

                def evac_ga(m, ba, bb):
                    s = SG[m % 3]
                    P.add('act', lambda e: e.activation(out=tp_f(s, T), in_=ps[ba][:, :T], func=AF.Sigmoid),
                          reads=PS(ba), writes=TP(s, 2))
                    P.add('dve', lambda e: e.tensor_tensor(out=rg_f(2 * m, T), in0=tp_f(s, T), in1=ps[bb][:, :T], op=ALU.mult),
                          reads=TP(s, 2) + PS(bb), writes=RG(2 * m, 2))
                dual_slab_pass(T, "gaao", 8, lambda k: rg_b(32 + k, T), lambda k: RG(32 + k), evac_ga)
            for j in range(16):
                sl, sres = load_slab(f"conv{j}")
                s3 = sl.rearrange("p (k n) -> p k n", k=16)
                base = 3 * (j % 2)
                nmat = 2 if halo else 3
                for a in range(nmat):
                    for k in range(NCH):
                        P.add('pe', lambda e, base=base, a=a, k=k, s3=s3: e.matmul(
                            ps[base + a][:, :T], s3[:, k, a * 128:(a + 1) * 128], uc_(k, T), start=(k == 0), stop=(k == NCH - 1)),
                            reads=sres + [('u', k)], writes=PS(base + a))
                s = j % 2
                UC = 3 * s
                TC = 6 + 2 * s
                CV = 10 + 2 * s
                P.add('dve', lambda e, UC=UC, j=j: e.tensor_copy(out=tp_f(UC, 2), in_=uch[:, 2 * j:2 * j + 2]),
                      reads=[('uch', j)], writes=TP(UC, 3))
                P.add('act', lambda e, TC=TC, base=base: e.activation(out=tp_f(TC, T), in_=ps[base][:, :T], func=AF.Copy),
                      reads=PS(base), writes=TP(TC, 2))
                P.add('dve', lambda e, UC=UC, TC=TC, base=base: e.tensor_tensor(
                    out=tp_f(UC, 2 + T, 2), in0=tp_f(TC, T), in1=ps[base + 1][:, :T], op=ALU.mult),
                    reads=TP(TC, 2) + PS(base + 1), writes=TP(UC, 3))
                P.add('dve', lambda e, UC=UC, j=j: e.tensor_copy(out=uch[:, 2 * j:2 * j + 2], in_=tp_f(UC, T + 2, T)),
                      reads=TP(UC, 3), writes=[('uch', j)])
                if not halo:
                    w0 = cs[:, 64 + j:64 + j + 1]
                    w1 = cs[:, 80 + j:80 + j + 1]
                    w2 = cs[:, 96 + j:96 + j + 1]
                    P.add('dve', lambda e, UC=UC, CV=CV, w2=w2: e.tensor_scalar(
                        out=tp_f(CV, T), in0=tp_f(UC, 2 + T, 2), scalar1=w2, scalar2=None, op0=ALU.mult),
                        reads=TP(UC, 3) + ['cs'], writes=TP(CV, 2))
                    P.add('dve', lambda e, UC=UC, CV=CV, w1=w1: e.scalar_tensor_tensor(
                        out=tp_f(CV, T), in0=tp_f(UC, 1 + T, 1), scalar=w1, in1=tp_f(CV, T), op0=ALU.mult, op1=ALU.add),
                        reads=TP(UC, 3) + TP(CV, 2) + ['cs'], writes=TP(CV, 2))
                    P.add('dve', lambda e, UC=UC, CV=CV, w0=w0: e.scalar_tensor_tensor(
                        out=tp_f(CV, T), in0=tp_f(UC, T, 0), scalar=w0, in1=tp_f(CV, T), op0=ALU.mult, op1=ALU.add),
                        reads=TP(UC, 3) + TP(CV, 2) + ['cs'], writes=TP(CV, 2))
                    P.add('dve', lambda e, CV=CV, base=base, j=j: e.tensor_tensor(
                        out=rg_b(32 + j, T), in0=tp_f(CV, T), in1=ps[base + 2][:, :T], op=ALU.mult),
                        reads=TP(CV, 2) + PS(base + 2), writes=RG(32 + j))
            if not halo:
                SG = [0, 2, 4]
                T2 = [14, 16]

                def evac_gc(m, ba, bb):
                    s = SG[m % 3]
                    t2 = T2[m % 2]
                    P.add('act', lambda e: e.activation(out=tp_f(s, T), in_=ps[ba][:, :T], func=AF.Sigmoid),
                          reads=PS(ba), writes=TP(s, 2))
                    P.add('dve', lambda e: e.tensor_tensor(out=tp_f(t2, T), in0=tp_f(s, T), in1=ps[bb][:, :T], op=ALU.mult),
                          reads=TP(s, 2) + PS(bb), writes=TP(t2, 2))
                    P.add('dve', lambda e: e.tensor_tensor(out=rg_b(48 + m, T), in0=tp_f(t2, T), in1=rg_f(2 * m, T), op=ALU.add),
                          reads=TP(t2, 2) + RG(2 * m, 2), writes=RG(48 + m))
                dual_slab_pass(T, "gcco", 8, lambda k: rg_b(32 + k, T), lambda k: RG(32 + k), evac_gc)
                for i in range(4):
                    sl, sres = load_slab(f"wo{i}")
                    s3 = sl.rearrange("p (k n) -> p k n", k=16)
                    for j in range(4):
                        m = 4 * i + j
                        bank = rot4()
                        for k in range(NCH):
                            P.add('pe', lambda e, bank=bank, k=k, j=j, s3=s3: e.matmul(
                                ps[bank][:, :T], s3[:, k, j * 128:(j + 1) * 128], rg_b(48 + k, T), start=(k == 0), stop=(k == NCH - 1)),
                                reads=sres + RG(48 + k), writes=PS(bank))
                        P.add('dve', lambda e, bank=bank, m=m: e.tensor_tensor(
                            out=hc(m, T), in0=ps[bank][:, :T], in1=hc(m, T), op=ALU.add),
                            reads=PS(bank) + [('h', m)], writes=[('h', m)])
            P.add('act', lambda e: e.activation(out=kA3[:, :, 0:128], in_=kA3[:, :, T:T + 128], func=AF.Copy),
                  reads=[('kA', nb)], writes=[('kA', 0)])
            P.add('act', lambda e: e.activation(out=kB3[:, :, 0:128], in_=kB3[:, :, T:T + 128], func=AF.Copy),
                  reads=[('kB', nb)], writes=[('kB', 0)])
            P.add('act', lambda e: e.activation(out=V3[:, 0, :], in_=V3[:, nb, :], func=AF.Copy),
                  reads=[('V', nb)], writes=[('V', 0)])

        def final(T, orow):
            nb = T // 128
            norm(T, 3, final=True)
            for b in range(nb):
                slot = state['ost'] % 2
                state['ost'] += 1
                cell = 8 * slot
                for cg in range(4):
                    bank = rott()
                    for j in range(4):
                        c = 4 * cg + j
                        P.add('pe', lambda e, bank=bank, j=j, c=c, b=b: e.transpose(
                            ps[bank][:, j * 128:(j + 1) * 128], hc(c, (b + 1) * 128, b * 128), ident),
                            reads=[('h', c), 'cs'], writes=PS(bank))
                    dst = rg_f(cell, (cg + 1) * 512, cg * 512)
                    if cg % 2 == 0:
                        fn = lambda e, bank=bank, dst=dst: e.activation(out=dst, in_=ps[bank][:, :], func=AF.Copy)
                        eng = 'act'
                    else:
                        fn = lambda e, bank=bank, dst=dst: e.tensor_copy(out=dst, in_=ps[bank][:, :])
                        eng = 'dve'
                    P.add(eng, fn, reads=PS(bank), writes=RG(cell + 2 * cg, 2))
                r0 = orow + b * 128
                od = ('outd', state['outd'])
                state['outd'] += 1
                P.add('act', lambda e, r0=r0, cell=cell: e.dma_start(out=out[r0:r0 + 128, :], in_=rg_f(cell, 2048)),
                      reads=RG(cell, 8), writes=[od], dma_key=('ost', slot))

        P.epoch = 0
        load_x(0, 128)
        ffn(128, 1, 0)
        mixer(128, True, False)
        for ti in range(nt):
            P.epoch = 1 + ti
            load_x(128 + ti * TT, TT)
            ffn(TT, 1, 0)
            mixer(TT, False, ti == 0)
            ffn(TT, 2, 2)
            final(TT, ti * TT)
        P.add('act', None, reads=[('outd', i) for i in range(state['outd'])])
        P.emit(nc)
        nc._prog_stats = (P.nops, P.nsems)
    return nc


_NC_CACHE = {}


def prepare_inputs(inputs, nt=8, cores=None):
    inp = {k: np.asarray(v) for k, v in inputs.items()}
    wslab = build_wslab(inp)
    x = inp["x"]
    in_maps = []
    if cores is None:
        cores = list(range(NCORES))
    for cid in cores:
        bi, half = cid // 2, cid % 2
        start = half * TOK_CORE
        xs = np.zeros((128 + nt * TT, D), np.float32)
        if start > 0:
            xs[0:128] = x[bi, start - 128:start]
        xs[128:] = x[bi, start:start + nt * TT]
        in_maps.append({"xs": xs, "wslab": wslab, "consts": build_consts(inp, start == 0)})
    return in_maps


def kernel(**inputs):
    nt = 8
    if nt not in _NC_CACHE:
        _NC_CACHE[nt] = build_program(nt)
    nc = _NC_CACHE[nt]
    in_maps = prepare_inputs(inputs, nt)
    res = run_bass_kernel_spmd(nc, in_maps, core_ids=list(range(NCORES)))
    outp = np.empty((BATCH, SEQ, D), np.float32)
    for cid in range(NCORES):
        bi, half = cid // 2, cid % 2
        outp[bi, half * TOK_CORE:(half + 1) * TOK_CORE] = res.results[cid]["out"]
    return outp
```
